# Optimizing a Trainium2 kernel written in Bass

```python
import math
import jax, jax.numpy as jnp
from jax import lax
import numpy as np

D_MODEL = 1024
BATCH = 8
SEQ = 2048
DEPTH = 4

HEAD_DIM = 64
N_HEADS = 4
MIX_W = N_HEADS * HEAD_DIM
N_BRANCH = 4
Q_BLOCK = 128
EPS = 1e-6
NEG = -1e30
FORCE = 1e9
MLA_Q_LORA = 384
MLA_KV_LORA = 128
MLA_NOPE = 64
MLA_ROPE = 32
MLA_V = 64
MLA_QK = MLA_NOPE + MLA_ROPE
ROPE_THETA = 10000.0
NSA_CMP_LEN = 32
NSA_CMP_STRIDE = 16
NSA_SEL_LEN = 64
NSA_TOP_N = 16
NSA_WINDOW = 512
REL_BUCKETS = 32
REL_MAX_DIST = 128
D_FF = 4 * D_MODEL
PLE_DIM = 256
IN_WIDTHS = ((MIX_W,) * 3
             + (MLA_Q_LORA, MLA_KV_LORA, MLA_ROPE)
             + (MIX_W,) + (HEAD_DIM,) * 6 + (3 * N_HEADS,)
             + (MIX_W,) * 3 + (N_HEADS,)
             + (N_BRANCH * D_MODEL,))
N_IN = sum(IN_WIDTHS)

kernel_name = 'hybrid_sb_mla_nsa_fox_block'


def rmsnorm(x, g):
    x32 = x.astype(jnp.float32)
    y = x32 * lax.rsqrt(jnp.mean(x32 * x32, axis=-1, keepdims=True) + EPS)
    return (y * g.astype(jnp.float32)).astype(x.dtype)


def to_heads(t):
    B, S, _ = t.shape
    return t.reshape(B, S, N_HEADS, -1).transpose(0, 2, 1, 3)


def from_heads(t):
    B, H, S, Dh = t.shape
    return t.transpose(0, 2, 1, 3).reshape(B, S, H * Dh)


def unblock(o):
    nb, B, H, Qb, Dv = o.shape
    return o.transpose(1, 2, 0, 3, 4).reshape(B, H, nb * Qb, Dv)


def rel_bucket(dist):
    max_exact = REL_BUCKETS // 2
    d = jnp.maximum(dist, 0)
    large = max_exact + (jnp.log(jnp.maximum(d, 1).astype(jnp.float32) / max_exact)
                         / math.log(REL_MAX_DIST / max_exact)
                         * (REL_BUCKETS - max_exact)).astype(jnp.int32)
    large = jnp.minimum(large, REL_BUCKETS - 1)
    return jnp.where(d < max_exact, d, large)


def rope_tables(S):
    half = MLA_ROPE // 2
    inv = jnp.exp(-math.log(ROPE_THETA) * jnp.arange(half, dtype=jnp.float32) / half)
    ang = jnp.arange(S, dtype=jnp.float32)[:, None] * inv[None, :]
    return jnp.cos(ang), jnp.sin(ang)


def apply_rope(t, cos, sin):
    half = t.shape[-1] // 2
    t1, t2 = t[..., :half], t[..., half:]
    c = cos[None, :, None, :].astype(t.dtype)
    s = sin[None, :, None, :].astype(t.dtype)
    return jnp.concatenate([t1 * c - t2 * s, t2 * c + t1 * s], axis=-1)


def causal_softmax_attention(q, k, v, logit_bias):
    B, H, S, Dk = q.shape
    scale = Dk ** -0.5
    kpos = jnp.arange(S)

    def block(i):
        start = i * Q_BLOCK
        qb = lax.dynamic_slice_in_dim(q, start, Q_BLOCK, axis=2)
        s = jnp.einsum('bhqd,bhkd->bhqk', qb, k).astype(jnp.float32) * scale
        if logit_bias is not None:
            s = s + logit_bias(start)
        qpos = start + jnp.arange(Q_BLOCK)
        s = jnp.where(kpos[None, :] <= qpos[:, None], s, -jnp.inf)
        return jnp.einsum('bhqk,bhkd->bhqd', jax.nn.softmax(s, axis=-1).astype(v.dtype), v)

    return unblock(lax.map(block, jnp.arange(S // Q_BLOCK)))


def stick_breaking_attention(q, k, v):
    B, H, S, Dh = q.shape
    scale = Dh ** -0.5
    kpos = jnp.arange(S)

    def block(i):
        start = i * Q_BLOCK
        qb = lax.dynamic_slice_in_dim(q, start, Q_BLOCK, axis=2)
        z = jnp.einsum('bhqd,bhkd->bhqk', qb, k).astype(jnp.float32) * scale
        qpos = start + jnp.arange(Q_BLOCK)
        past = kpos[None, :] < qpos[:, None]
        log_1m = jnp.where(past, jax.nn.log_sigmoid(-z), 0.0)
        between = lax.cumsum(log_1m, axis=3, reverse=True) - log_1m
        w = jnp.where(past, jnp.exp(jax.nn.log_sigmoid(z) + between), 0.0)
        return jnp.einsum('bhqk,bhkd->bhqd', w.astype(v.dtype), v)

    return unblock(lax.map(block, jnp.arange(S // Q_BLOCK)))


def mla_mixer(c_q, c_kv, k_rope, cq_g, ckv_g, w_uq, w_ukv, qn_g, kn_g):
    B, S, _ = c_q.shape
    q = (rmsnorm(c_q, cq_g) @ w_uq).reshape(B, S, N_HEADS, MLA_QK)
    kv = (rmsnorm(c_kv, ckv_g) @ w_ukv).reshape(B, S, N_HEADS, MLA_NOPE + MLA_V)
    k_nope, v = kv[..., :MLA_NOPE], kv[..., MLA_NOPE:]
    k = jnp.concatenate([k_nope, jnp.broadcast_to(k_rope[:, :, None, :], (B, S, N_HEADS, MLA_ROPE))], axis=-1)
    q = rmsnorm(q, qn_g)
    k = rmsnorm(k, kn_g)
    cos, sin = rope_tables(S)
    q = jnp.concatenate([q[..., :MLA_NOPE], apply_rope(q[..., MLA_NOPE:], cos, sin)], axis=-1)
    k = jnp.concatenate([k[..., :MLA_NOPE], apply_rope(k[..., MLA_NOPE:], cos, sin)], axis=-1)
    o = causal_softmax_attention(q.transpose(0, 2, 1, 3), k.transpose(0, 2, 1, 3),
                                 v.transpose(0, 2, 1, 3), None)
    return from_heads(o)


def nsa_mixer(q, k_cmp, v_cmp, k_slc, v_slc, k_win, v_win, g_logit,
              pe_k, pe_v, w1_k, w2_k, w1_v, w2_v, qn_g, kn_g, rel_bias):
    B, S, _ = q.shape
    Dh = HEAD_DIM
    scale = Dh ** -0.5
    q = rmsnorm(to_heads(q), qn_g)
    tpos = jnp.arange(S)
    table = rel_bias.astype(jnp.float32)

    n_cmp = (S - NSA_CMP_LEN) // NSA_CMP_STRIDE + 1
    starts = jnp.arange(n_cmp) * NSA_CMP_STRIDE
    gidx = starts[:, None] + jnp.arange(NSA_CMP_LEN)[None, :]

    def compress(t, pe, w1, w2):
        blocks = (t[:, gidx, :] + pe).reshape(B, n_cmp, NSA_CMP_LEN * Dh)
        return jax.nn.silu(blocks @ w1) @ w2

    kc = rmsnorm(compress(k_cmp, pe_k, w1_k, w2_k), kn_g[0])
    vc = compress(v_cmp, pe_v, w1_v, w2_v)
    cmp_dist = tpos[:, None] - (starts + NSA_CMP_LEN - 1)[None, :]
    cmp_valid = cmp_dist >= 0
    s_c = (jnp.einsum('bhsd,bcd->bhsc', q, kc).astype(jnp.float32) * scale
           + table[rel_bucket(cmp_dist)].transpose(2, 0, 1)[None])
    p_c = jax.nn.softmax(jnp.where(cmp_valid, s_c, NEG), axis=-1)
    p_c = jnp.where(cmp_valid, p_c, 0.0)
    o_cmp = jnp.einsum('bhsc,bcd->bhsd', p_c.astype(vc.dtype), vc)

    n_sel = S // NSA_SEL_LEN
    c0 = starts[:, None]
    j0 = (jnp.arange(n_sel) * NSA_SEL_LEN)[None, :]
    overlap = ((c0 < j0 + NSA_SEL_LEN) & (c0 + NSA_CMP_LEN > j0)).astype(jnp.float32)
    imp = jnp.einsum('bhsc,cj->bsj', p_c, overlap)
    blk = jnp.arange(n_sel)[None, :]
    cur = (tpos // NSA_SEL_LEN)[:, None]
    forced = (blk == 0) | (blk == cur) | (blk == cur - 1)
    imp = jnp.where(forced, FORCE, imp)
    imp = jnp.where(blk <= cur, imp, -FORCE)
    top_n = min(NSA_TOP_N, n_sel)
    _, sel = lax.top_k(imp, top_n)

    ks = rmsnorm(k_slc, kn_g[1])
    kw = rmsnorm(k_win, kn_g[2])
    kw_pad = jnp.pad(kw, ((0, 0), (NSA_WINDOW, 0), (0, 0)))
    vw_pad = jnp.pad(v_win, ((0, 0), (NSA_WINDOW, 0), (0, 0)))
    bidx = jnp.arange(B)[:, None, None]
    offs = jnp.arange(NSA_SEL_LEN)
    n_keys = top_n * NSA_SEL_LEN

    def block(i):
        start = i * Q_BLOCK
        qb = lax.dynamic_slice_in_dim(q, start, Q_BLOCK, axis=2)
        qpos = start + jnp.arange(Q_BLOCK)
        sb = lax.dynamic_slice_in_dim(sel, start, Q_BLOCK, axis=1)
        kpos = (sb[..., None] * NSA_SEL_LEN + offs).reshape(B, Q_BLOCK, n_keys)
        kg = ks[bidx, kpos]
        vg = v_slc[bidx, kpos]
        dist = qpos[None, :, None] - kpos
        s = (jnp.einsum('bhqd,bqkd->bhqk', qb, kg).astype(jnp.float32) * scale
             + table[rel_bucket(dist)].transpose(0, 3, 1, 2))
        s = jnp.where((dist >= 0)[:, None], s, -jnp.inf)
        o_s = jnp.einsum('bhqk,bqkd->bhqd', jax.nn.softmax(s, axis=-1).astype(vg.dtype), vg)
        kb = lax.dynamic_slice_in_dim(kw_pad, start, NSA_WINDOW + Q_BLOCK, axis=1)
        vb = lax.dynamic_slice_in_dim(vw_pad, start, NSA_WINDOW + Q_BLOCK, axis=1)
        wpos = start - NSA_WINDOW + jnp.arange(NSA_WINDOW + Q_BLOCK)
        wdist = qpos[:, None] - wpos[None, :]
        wmask = (wdist >= 0) & (wdist < NSA_WINDOW) & (wpos[None, :] >= 0)
        s_w = (jnp.einsum('bhqd,bkd->bhqk', qb, kb).astype(jnp.float32) * scale
               + table[rel_bucket(wdist)].transpose(2, 0, 1)[None])
        s_w = jnp.where(wmask, s_w, -jnp.inf)
        o_w = jnp.einsum('bhqk,bkd->bhqd', jax.nn.softmax(s_w, axis=-1).astype(vb.dtype), vb)
        return o_s, o_w

    o_slc, o_win = lax.map(block, jnp.arange(S // Q_BLOCK))
    o_slc = unblock(o_slc)
    o_win = unblock(o_win)
    g = jax.nn.sigmoid(g_logit.astype(jnp.float32)).reshape(B, S, 3, N_HEADS).transpose(2, 0, 3, 1)[..., None]
    o = g[0] * o_cmp + g[1] * o_slc + g[2] * o_win
    return from_heads(o.astype(q.dtype))


def fox_mixer(q, k, v, f_logit, f_bias, qn_g, kn_g):
    q = rmsnorm(to_heads(q), qn_g)
    k = rmsnorm(to_heads(k), kn_g)
    v = to_heads(v)
    log_f = jax.nn.log_sigmoid(f_logit.astype(jnp.float32) + f_bias.astype(jnp.float32))
    cum = lax.cumsum(log_f, axis=1).transpose(0, 2, 1)

    def decay_bias(start):
        cq = lax.dynamic_slice_in_dim(cum, start, Q_BLOCK, axis=2)
        return cq[..., :, None] - cum[..., None, :]

    return from_heads(causal_softmax_attention(q, k, v, decay_bias))


def setup_inputs(seed: int = 0) -> dict:
    key = jax.random.key(seed)
    keys = list(jax.random.split(key, 40))
    f32 = jnp.float32

    def nrm(shape, scale):
        return scale * jax.random.normal(keys.pop(), shape, f32)

    def gain(shape):
        return 1.0 + 0.05 * jax.random.normal(keys.pop(), shape, f32)

    L = DEPTH
    cmp_in = NSA_CMP_LEN * HEAD_DIM
    return {
        'x': nrm((BATCH, SEQ, D_MODEL), 1.0),
        'p': nrm((DEPTH, BATCH, SEQ, PLE_DIM), 1.0),
        'rel_bias': nrm((REL_BUCKETS, N_HEADS), 0.5),
        'norm_mix_g': gain((L, D_MODEL)),
        'w_in': nrm((L, D_MODEL, N_IN), D_MODEL ** -0.5),
        'mla_cq_norm_g': gain((L, MLA_Q_LORA)),
        'mla_ckv_norm_g': gain((L, MLA_KV_LORA)),
        'mla_w_uq': nrm((L, MLA_Q_LORA, N_HEADS * MLA_QK), MLA_Q_LORA ** -0.5),
        'mla_w_ukv': nrm((L, MLA_KV_LORA, N_HEADS * (MLA_NOPE + MLA_V)), MLA_KV_LORA ** -0.5),
        'mla_qn_g': gain((L, MLA_QK)),
        'mla_kn_g': gain((L, MLA_QK)),
        'nsa_pe_k': nrm((L, NSA_CMP_LEN, HEAD_DIM), 0.1),
        'nsa_pe_v': nrm((L, NSA_CMP_LEN, HEAD_DIM), 0.1),
        'nsa_w1_k': nrm((L, cmp_in, HEAD_DIM), cmp_in ** -0.5),
        'nsa_w2_k': nrm((L, HEAD_DIM, HEAD_DIM), HEAD_DIM ** -0.5),
        'nsa_w1_v': nrm((L, cmp_in, HEAD_DIM), cmp_in ** -0.5),
        'nsa_w2_v': nrm((L, HEAD_DIM, HEAD_DIM), HEAD_DIM ** -0.5),
        'nsa_qn_g': gain((L, HEAD_DIM)),
        'nsa_kn_g': gain((L, 3, HEAD_DIM)),
        'fox_f_bias': 2.0 + nrm((L, N_HEADS), 0.5),
        'fox_qn_g': gain((L, HEAD_DIM)),
        'fox_kn_g': gain((L, HEAD_DIM)),
        'w_branch': nrm((L, N_BRANCH, MIX_W, D_MODEL), MIX_W ** -0.5),
        'w_o': nrm((L, D_MODEL, D_MODEL), D_MODEL ** -0.5),
        'norm_mlp_g': gain((L, D_MODEL)),
        'w_mlp_up': nrm((L, D_MODEL, D_FF), D_MODEL ** -0.5),
        'w_mlp_down': nrm((L, D_FF, D_MODEL), D_FF ** -0.5),
        'norm_ple_g': gain((L, D_MODEL)),
        'w_ple_gate': nrm((L, D_MODEL, D_MODEL), D_MODEL ** -0.5),
        'w_ple_proj': nrm((L, PLE_DIM, D_MODEL), PLE_DIM ** -0.5),
    }


def reference(x, p, rel_bias, norm_mix_g, w_in, mla_cq_norm_g, mla_ckv_norm_g,
              mla_w_uq, mla_w_ukv, mla_qn_g, mla_kn_g, nsa_pe_k, nsa_pe_v,
              nsa_w1_k, nsa_w2_k, nsa_w1_v, nsa_w2_v, nsa_qn_g, nsa_kn_g,
              fox_f_bias, fox_qn_g, fox_kn_g, w_branch, w_o, norm_mlp_g,
              w_mlp_up, w_mlp_down, norm_ple_g, w_ple_gate, w_ple_proj):
    B, S, _ = x.shape
    offsets = np.cumsum(IN_WIDTHS)[:-1].tolist()
    for i in range(DEPTH):
        h = rmsnorm(x, norm_mix_g[i])
        (sb_q, sb_k, sb_v, mla_cq, mla_ckv, mla_kr,
         nsa_q, nsa_kc, nsa_vc, nsa_ks, nsa_vs, nsa_kw, nsa_vw, nsa_g,
         fox_q, fox_k, fox_v, fox_f, gate_logits) = jnp.split(h @ w_in[i], offsets, axis=-1)

        y_sb = from_heads(stick_breaking_attention(to_heads(sb_q), to_heads(sb_k), to_heads(sb_v)))
        y_mla = mla_mixer(mla_cq, mla_ckv, mla_kr, mla_cq_norm_g[i], mla_ckv_norm_g[i],
                          mla_w_uq[i], mla_w_ukv[i], mla_qn_g[i], mla_kn_g[i])
        y_nsa = nsa_mixer(nsa_q, nsa_kc, nsa_vc, nsa_ks, nsa_vs, nsa_kw, nsa_vw, nsa_g,
                          nsa_pe_k[i], nsa_pe_v[i], nsa_w1_k[i], nsa_w2_k[i],
                          nsa_w1_v[i], nsa_w2_v[i], nsa_qn_g[i], nsa_kn_g[i], rel_bias)
        y_fox = fox_mixer(fox_q, fox_k, fox_v, fox_f, fox_f_bias[i], fox_qn_g[i], fox_kn_g[i])

        branches = jnp.stack([y_sb, y_mla, y_nsa, y_fox], axis=2)
        widened = jnp.einsum('bsnc,ncd->bsnd', branches, w_branch[i])
        gates = jax.nn.sigmoid(gate_logits.reshape(B, S, N_BRANCH, D_MODEL))
        x = x + jnp.einsum('bsnd,de->bse', gates * widened, w_o[i])

        h2 = rmsnorm(x, norm_mlp_g[i])
        x = x + jnp.square(jax.nn.relu(h2 @ w_mlp_up[i])) @ w_mlp_down[i]

        ple_gate = jax.nn.sigmoid(rmsnorm(x, norm_ple_g[i]) @ w_ple_gate[i])
        x = x + ple_gate * (p[i] @ w_ple_proj[i])
    return x
```

```python
import math
from contextlib import ExitStack
import numpy as np
import ml_dtypes
import concourse.bass as bass
import concourse.mybir as mybir
from concourse.bass_utils import run_bass_kernel_spmd

F32 = mybir.dt.float32
BF16 = mybir.dt.bfloat16
AF = mybir.ActivationFunctionType
ALU = mybir.AluOpType
AX = mybir.AxisListType

S = 2048
D = 1024
NT = 16
DEPTH = 4
N_IN = 6832
NEGM = -30000.0
EPS = 1e-6
VOFF = 271
VLEN = 1024

ENGS = ("pe", "act", "dve", "pool", "sp")
NOSELF = ("pe",)
N_DMA_SEMS = 12


class _Op:
    __slots__ = ("eng", "fn", "deps", "dma", "sig", "id")

    def __init__(self, eng, fn, deps, dma, id):
        self.eng, self.fn, self.deps, self.dma, self.id = eng, fn, deps, dma, id
        self.sig = None


class Prog:
    def __init__(self, nc):
        self.nc = nc
        self.ops = []
        self.w = {}
        self.r = {}
        self.bar = set()
        self.last = {}
        self.dmas_since = []

    def op(self, eng, fn, reads=(), writes=(), dma=False, nobar=False):
        pk = [k for k in reads if k == "psb" or (isinstance(k, tuple) and k and k[0] == "ps")]
        if pk:
            writes = list(writes) + [k for k in pk if k not in writes]
        deps = set() if nobar else set(self.bar)
        for k in reads:
            if k in self.w:
                deps.add(self.w[k])
        for k in writes:
            if k in self.w:
                deps.add(self.w[k])
            for i in self.r.get(k, ()):
                deps.add(i)
        i = len(self.ops)
        for k in reads:
            self.r.setdefault(k, []).append(i)
        for k in writes:
            self.w[k] = i
            self.r[k] = []
        self.ops.append(_Op(eng, fn, deps, dma, i))
        if dma:
            self.dmas_since.append(i)
        else:
            self.last[eng] = i
        return i

    def barrier(self):
        self.bar = set(self.last.values()) | set(self.dmas_since)
        self.dmas_since = []

    def emit(self, stack, final_wait_ops=()):
        nc = self.nc
        ops = self.ops
        needed = set()
        for o in ops:
            for d in o.deps:
                po = ops[d]
                if po.dma or po.eng != o.eng or o.eng not in NOSELF:
                    needed.add(d)
        for d in final_wait_ops:
            needed.add(d)
        esem = {e: stack.enter_context(nc.semaphore("s_" + e)) for e in ENGS}
        dsem = {e: [stack.enter_context(nc.semaphore("d_%s_%d" % (e, i))) for i in range(N_DMA_SEMS)]
                for e in ("sp", "act", "pool")}
        ecount = {e: 0 for e in ENGS}
        dcount = {e: [0] * N_DMA_SEMS for e in dsem}
        drr = {e: 0 for e in dsem}
        for o in ops:
            if o.dma:
                k = drr[o.eng]
                drr[o.eng] = (k + 1) % N_DMA_SEMS
                prev = dcount[o.eng][k]
                dcount[o.eng][k] = prev + 16
                o.sig = ("d", o.eng, k, prev + 16, prev)
            elif o.id in needed:
                ecount[o.eng] += 1
                o.sig = ("e", o.eng, ecount[o.eng])
        by_eng = {e: [o for o in ops if o.eng == e] for e in ENGS}
        final_ops = [ops[i] for i in final_wait_ops]

        def run_engine(ename, eng):
            seen = {}

            def wait(sig):
                if sig[0] == "d":
                    sem = dsem[sig[1]][sig[2]]
                    key = ("d", sig[1], sig[2])
                    val = sig[3]
                else:
                    sem = esem[sig[1]]
                    key = ("e", sig[1])
                    val = sig[2]
                if seen.get(key, 0) >= val:
                    return
                seen[key] = val
                eng.wait_ge(sem, val)

            for o in by_eng[ename]:
                for d in sorted(o.deps):
                    po = ops[d]
                    if po.eng == ename and not po.dma and ename in NOSELF:
                        continue
                    wait(po.sig)
                if o.dma:
                    _, q, k, val, prev = o.sig
                    if prev > 0:
                        wait(("d", q, k, prev, 0))
                    o.fn(eng).then_inc(dsem[q][k], 16)
                else:
                    ins = o.fn(eng)
                    if o.sig is not None:
                        ins.then_inc(esem[ename], 1)
            if ename == "sp":
                for fo in final_ops:
                    wait(fo.sig)

        with nc.Block() as block:
            @block.tensor
            def _(e):
                run_engine("pe", e)

            @block.scalar
            def _(e):
                run_engine("act", e)

            @block.vector
            def _(e):
                run_engine("dve", e)

            @block.gpsimd
            def _(e):
                run_engine("pool", e)

            @block.sync
            def _(e):
                run_engine("sp", e)


def _rel_bucket(d):
    d = np.maximum(d, 0)
    large = 16 + (np.log(np.maximum(d, 1).astype(np.float32) / 16) / math.log(128 / 16) * 16).astype(np.int32)
    large = np.minimum(large, 31)
    return np.where(d < 16, d, large)


def _bf(a):
    return np.asarray(a, dtype=np.float32).astype(ml_dtypes.bfloat16)


def host_consts():
    c = {}
    p = np.arange(128)[:, None]
    j = np.arange(128)[None, :]
    c["ident_bf"] = _bf(np.eye(128))
    c["ident_f"] = np.eye(128, dtype=np.float32)
    c["antij_bf"] = _bf((p + j) == 127)
    c["minc_bf"] = _bf(p >= j)
    c["mcomp_bf"] = _bf(p < j)
    c["utri_f"] = (p <= j).astype(np.float32)
    c["ones_f"] = np.ones((128, 128), np.float32)
    c["zeros_bf"] = _bf(np.zeros((128, 128)))
    jj = np.arange(1024)[None, :]
    c["wc_bf"] = _bf(np.where(jj[:, :512] < p, NEGM, 0.0))
    c["wm01_bf"] = _bf((jj[:, :512] > p).astype(np.float32))
    c["wu_bf"] = _bf(np.where(jj - p >= 512, NEGM, 0.0))
    u = np.arange(VLEN)
    bk = _rel_bucket(u - VOFF)
    oh = np.zeros((32, VLEN), np.float32)
    oh[bk, u] = 1.0
    oh[:, u < VOFF] = 0.0
    c["oh_f"] = oh
    c["vmask_f"] = np.tile(np.where(u < VOFF, NEGM, 0.0).astype(np.float32)[None, :], (4, 1))
    s = np.arange(S)
    c["ex_bf"] = _bf((s[None, :] // 64) == np.arange(128)[:, None])
    sel = np.zeros((128, 16, 127), np.float32)
    cc = np.arange(127)
    for qt in range(16):
        for rp in range(32):
            ci = 8 * qt + 15 - rp
            if 0 <= ci < 127:
                sel[rp, qt, ci] = 1.0
        sel[32, qt, cc < 8 * qt - 16] = 1.0
        sel[33, qt, cc > 8 * qt + 15] = 1.0
    c["sel_bf"] = _bf(sel)
    ov = np.zeros((127, 34), np.float32)
    c0 = (np.arange(127) * 16)[:, None]
    j0 = (np.arange(32) * 64)[None, :]
    ov[:, :32] = ((c0 < j0 + 64) & (c0 + 32 > j0)).astype(np.float32)
    ov[:, 32] = 1.0
    c["ovx_f"] = ov
    m2 = np.zeros((128, 8, 32), np.float32)
    blk = np.arange(32)[None, :]
    for q8 in range(8):
        qt = q8 + 8
        cur = ((qt * 128 + np.arange(128)) // 64)[:, None]
        forced = (blk == 0) | (blk == cur) | (blk == cur - 1)
        m = np.where(forced, 1e9, 0.0)
        m = np.where(blk <= cur, m, -1e9)
        m2[:, q8, :] = m
    c["m2_f"] = m2
    half = 16
    inv = np.exp(-math.log(10000.0) * np.arange(half, dtype=np.float32) / half).astype(np.float32)
    ang = np.arange(S, dtype=np.float32)[:, None] * inv[None, :]
    cs = np.stack([np.cos(ang), np.sin(ang)], axis=1).astype(np.float32)
    c["cossin_f"] = np.ascontiguousarray(cs.reshape(16, 128, 2, 16).transpose(1, 0, 2, 3))
    es = np.zeros((12, 12, 64), np.float32)
    for i in range(12):
        es[i, i, :] = 1.0
    c["esel_bf"] = _bf(es)
    return c


CONST_SHAPES = None


class _Stop(Exception):
    pass


def build(n_layers=DEPTH, dbg=False, stop=None):
    nc = bass.Bass("TRN2", target_bir_lowering=False)
    consts = host_consts()

    def din(name, shape, dt=F32):
        return nc.dram_tensor(name, list(shape), dt, kind="ExternalInput").ap()

    L = DEPTH
    x_d = din("x", [S, D])
    p_d = din("p", [L, S, 256])
    relb_d = din("rel_bias", [32, 4])
    g_mix_d = din("norm_mix_g", [L, D])
    w_in_d = din("w_in", [L, D, N_IN])
    cqg_d = din("mla_cq_norm_g", [L, 384])
    ckvg_d = din("mla_ckv_norm_g", [L, 128])
    wuq_d = din("mla_w_uq", [L, 384, 384])
    wukv_d = din("mla_w_ukv", [L, 128, 512])
    mqn_d = din("mla_qn_g", [L, 96])
    mkn_d = din("mla_kn_g", [L, 96])
    pek_d = din("nsa_pe_k", [L, 32, 64])
    pev_d = din("nsa_pe_v", [L, 32, 64])
    w1k_d = din("nsa_w1_k", [L, 2048, 64])
    w2k_d = din("nsa_w2_k", [L, 64, 64])
    w1v_d = din("nsa_w1_v", [L, 2048, 64])
    w2v_d = din("nsa_w2_v", [L, 64, 64])
    nqn_d = din("nsa_qn_g", [L, 64])
    nkn_d = din("nsa_kn_g", [L, 3, 64])
    ffb_d = din("fox_f_bias", [L, 4])
    fqn_d = din("fox_qn_g", [L, 64])
    fkn_d = din("fox_kn_g", [L, 64])
    wbr_d = din("w_branch", [L, 4, 256, D])
    wo_d = din("w_o", [L, D, D])
    g_mlp_d = din("norm_mlp_g", [L, D])
    wup_d = din("w_mlp_up", [L, D, 4096])
    wdn_d = din("w_mlp_down", [L, 4096, D])
    g_ple_d = din("norm_ple_g", [L, D])
    wpg_d = din("w_ple_gate", [L, D, D])
    wpp_d = din("w_ple_proj", [L, 256, D])
    cd = {}
    for k, v in consts.items():
        cd[k] = din("c_" + k, v.shape, BF16 if v.dtype == ml_dtypes.bfloat16 else F32)
    out_d = nc.dram_tensor("out", [S, D], F32, kind="ExternalOutput").ap()
    if dbg:
        dbg_y = nc.dram_tensor("dbg_y", [1024, S], BF16, kind="ExternalOutput")
        yT_t = dbg_y
    else:
        yT_t = nc.dram_tensor("yT_scr", [1024, S], BF16, kind="Internal")
    yT_d = yT_t.ap()
    vscr_t = nc.dram_tensor("v_scr", [4, VLEN], F32, kind="Internal")
    vscr_d = vscr_t.ap()

    st = ExitStack()
    with st:
        def sb(name, shape, dt):
            return st.enter_context(nc.sbuf_tensor("sb_" + name, list(shape), dt))

        def psum(name, shape, dt):
            return st.enter_context(nc.psum_tensor("pp_" + name, list(shape), dt))

        x = sb("x", [128, NT, D], F32)
        hT = sb("hT", [128, 8, S], BF16)
        R32 = sb("R32", [128, 16384], BF16)
        XB = sb("XB", [128, 6144], BF16)
        W0 = sb("W0", [128, 4096], BF16)
        W1 = sb("W1", [128, 4096], BF16)
        ident = sb("ident", [128, 128], BF16)
        identf = sb("identf", [128, 128], F32)
        antij = sb("antij", [128, 128], BF16)
        minc = sb("minc", [128, 128], BF16)
        mcomp = sb("mcomp", [128, 128], BF16)
        utri = sb("utri", [128, 128], F32)
        onesf = sb("onesf", [128, 128], F32)
        zeros = sb("zeros", [128, 128], BF16)
        wc = sb("wc", [128, 512], BF16)
        wm01 = sb("wm01", [128, 512], BF16)
        wu = sb("wu", [128, 1024], BF16)
        hb = sb("hb", [128, 4, 640], BF16)
        exm = sb("exm", [128, S], BF16)
        selm = sb("selm", [128, 16, 127], BF16)
        pat = sb("pat", [128, 4, 128], BF16)
        ovx = sb("ovx", [127, 34], F32)
        m2 = sb("m2", [128, 8, 32], F32)
        esel = sb("esel", [12, 12, 64], BF16)
        t31 = sb("t31", [128, 4], F32)
        PT = [sb("PT%d" % i, [128, 512], BF16) for i in range(2)]
        tmpE = sb("tmpE", [128, 512], F32)
        lpB = sb("lpB", [128, 1024], BF16)
        tmpF = sb("tmpF", [128, 1024], F32)
        stgB = [sb("stgB%d" % i, [128, 512], BF16) for i in range(2)]
        stat = sb("stat", [128, 64], F32)
        yst = [sb("yst%d" % i, [128, 512], BF16) for i in range(2)]
        impacc = sb("impacc", [128, 8, 32], F32)
        foxc = sb("foxc", [128, 2, NT, 4], F32)
        gT = sb("gT", [128, 32], F32)
        grow = sb("grow", [32, 128], F32)
        gb = sb("gb", [128, 608], F32)
        krb = sb("krb", [128, 16], F32)
        smallb = sb("smallb", [128, 256], BF16)
        smallf = sb("smallf", [128, 64], F32)
        nsasm = sb("nsasm", [128, 256], BF16)
        stat2 = sb("stat2", [128, 64], F32)
        junk = sb("junk", [128, 1024], BF16)
        PS = [psum("ps%d" % i, [128, 512], F32) for i in range(7)]
        PSB = psum("psb", [128, 512], F32)

        QA = R32[:, 0:4096].rearrange("p (a t) -> p a t", a=2)
        KA = R32[:, 4096:8192].rearrange("p (a t) -> p a t", a=2)
        KB = R32[:, 8192:12288].rearrange("p (a t) -> p a t", a=2)
        KBf = R32[:, 8192:12288].bitcast(F32)
        VA = R32[:, 12288:16384].rearrange("p (s a c) -> p s a c", s=NT, a=2)
        VA4 = R32[:, 12288:16384].rearrange("p (s a c) -> p s a c", s=NT, a=4)
        uT = R32[:, :].rearrange("p (a t) -> p a t", a=8)
        aT = R32[:, 0:8192].rearrange("p (a t) -> p a t", a=4)
        krope = tmpE[:, :].rearrange("p (a c) -> p a c", a=NT)
        cossin = lpB[:, :].bitcast(F32).rearrange("p (a b c) -> p a b c", a=NT, b=2)
        selbT = lpB[:, :]
        W0k = W0[:, :].rearrange("p (k c) -> p k c", k=8)
        W1k = W1[:, :].rearrange("p (k c) -> p k c", k=8)
        uacc = W1[:, :].bitcast(F32)

        P = Prog(nc)
        TB = [(PSB, "psb"), (PS[2], ("ps", 2)), (PS[3], ("ps", 3)), (PS[4], ("ps", 4))]

        def chk(name):
            if stop == name:
                raise _Stop()

        def MM(out, lhsT, rhs, start, stop, R, W):
            P.op("pe", lambda e: e.matmul(out, lhsT=lhsT, rhs=rhs, start=start, stop=stop,
                                          skip_group_check=True), R, W)

        def TRP(out, in_, idt, R, W):
            k = in_.shape[0]
            P.op("pe", lambda e: e.matmul(out, lhsT=in_, rhs=idt[0:k, 0:k], start=True, stop=True,
                                          skip_group_check=True), R, W)

        def ACT(out, in_, func, R, W, bias=None, scale=None):
            kw = {}
            if bias is not None:
                kw["bias"] = bias
            if scale is not None:
                kw["scale"] = scale
            P.op("act", lambda e: e.activation(out=out, in_=in_, func=func, **kw), R, W)

        def TT(eng, out, in0, in1, op, R, W):
            P.op(eng, lambda e: e.tensor_tensor(out=out, in0=in0, in1=in1, op=op), R, W)

        def TS(eng, out, in0, s1, s2, op0, op1, R, W):
            if op1 is None:
                P.op(eng, lambda e: e.tensor_scalar(out=out, in0=in0, scalar1=s1, scalar2=None, op0=op0), R, W)
            else:
                P.op(eng, lambda e: e.tensor_scalar(out=out, in0=in0, scalar1=s1, scalar2=s2, op0=op0, op1=op1), R, W)

        def CP(eng, out, in_, R, W):
            if eng == "act":
                P.op("act", lambda e: e.activation(out=out, in_=in_, func=AF.Copy), R, W)
            else:
                P.op(eng, lambda e: e.tensor_copy(out=out, in_=in_), R, W)

        def RED(out, in_, R, W):
            P.op("dve", lambda e: e.tensor_reduce(out=out, in_=in_, axis=AX.X, op=ALU.add), R, W)

        def RECIP(out, in_, R, W):
            P.op("dve", lambda e: e.reciprocal(out=out, in_=in_), R, W)

        def MEMSET(eng, ap, val, W):
            P.op(eng, lambda e: e.memset(ap, val), (), W)

        def DMA(q, out, in_, R, W, nobar=False):
            P.op(q, lambda e: e.dma_start(out=out, in_=in_), R, W, dma=True, nobar=nobar)

        def bc(ap, shape):
            return ap.to_broadcast(list(shape))

        def RSTD(ss_ap, w, key):
            keys = key if isinstance(key, list) else [key]
            ACT(ss_ap, ss_ap, AF.Ln, keys, keys, bias=EPS, scale=1.0 / w)
            ACT(ss_ap, ss_ap, AF.Exp, keys, keys, scale=-0.5)

        def ssq_A(srcs, base, rkeys):
            keys = []
            for i, src in enumerate(srcs):
                w = src.shape[-1]
                col = stat2[:, base + i:base + i + 1]
                k = ("s2", base + i)
                P.op("act", lambda e, src=src, col=col, w=w: e.activation(out=junk[:, 0:w], in_=src, func=AF.Square,
                                                                      accum_out=col), rkeys, [k])
                keys.append(k)
            return keys

        def norm_B(src3, nh, w, gain2, out3, ss, skeys, rkeys, wkeys):
            sq = tmpF[:, 0:nh * w].rearrange("p (h c) -> p h c", h=nh)
            TT("dve", sq, src3, bc(ss.unsqueeze(2), [128, nh, w]), ALU.mult, rkeys + skeys, ["tmpF"])
            TT("dve", out3, sq, bc(gain2.unsqueeze(1), [128, nh, w]), ALU.mult, ["tmpF", "gb"], wkeys)

        for nm, tl in (("ident_bf", ident), ("ident_f", identf), ("antij_bf", antij), ("minc_bf", minc),
                       ("mcomp_bf", mcomp), ("utri_f", utri), ("ones_f", onesf), ("zeros_bf", zeros),
                       ("wc_bf", wc), ("wm01_bf", wm01), ("wu_bf", wu), ("ex_bf", exm), ("sel_bf", selm),
                       ("ovx_f", ovx), ("m2_f", m2), ("esel_bf", esel)):
            DMA("sp", tl[:], cd[nm], [], [nm])
        for it in range(NT):
            DMA("sp", x[:, it, :], x_d[it * 128:(it + 1) * 128, :], [], [("x", it)])
        tab = smallf[0:32, 0:4]
        DMA("sp", tab, relb_d, [], ["tab"])
        ohs = tmpF[0:32, :]
        DMA("sp", ohs, cd["oh_f"], [], ["tmpF"])
        vm_sb = XB[0:4, 0:2048].bitcast(F32)
        DMA("sp", vm_sb[:], cd["vmask_f"], [], ["vm_sb"])
        for ch in range(2):
            MM(PS[ch][0:4, :], tab, ohs[:, ch * 512:(ch + 1) * 512], True, True, ["tab", "tmpF"], [("ps", ch)])
            TT("dve", vm_sb[:, ch * 512:(ch + 1) * 512], PS[ch][0:4, :], vm_sb[:, ch * 512:(ch + 1) * 512],
               ALU.add, [("ps", ch), "vm_sb"], ["vm_sb"])
        DMA("sp", vscr_d, vm_sb[:], ["vm_sb"], ["vscr"])
        hbf = tmpF[:, 0:640]
        for h in range(4):
            src = bass.AP(tensor=vscr_t, offset=h * VLEN + VOFF - 127, ap=[[1, 128], [1, 640]])
            DMA("sp", hbf, src, ["vscr"], ["tmpF"])
            CP("dve", hb[:, h, :], hbf, ["tmpF"], ["hb"])
        patf = tmpF[0:34, 0:512].rearrange("p (h t) -> p h t", h=4)
        MEMSET("dve", tmpF[32:34, 0:512], NEGM, ["tmpF"])
        for h in range(4):
            src = bass.AP(tensor=vscr_t, offset=h * VLEN, ap=[[16, 32], [1, 128]])
            DMA("sp", patf[0:32, h, :], src, ["vscr"], ["tmpF"])
            src2 = bass.AP(tensor=vscr_t, offset=h * VLEN + 800, ap=[[0, 1], [1, 128]])
            DMA("sp", patf[32:33, h, :], src2, ["vscr"], ["tmpF"])
        MEMSET("dve", pat[:], 0.0, ["pat"])
        CP("dve", pat[0:34], patf, ["tmpF"], ["pat"])
        DMA("sp", t31[:], relb_d[31:32, :].to_broadcast([128, 4]), [], ["t31"])
        P.barrier()

        wbufs = [W0k, W1k]
        wstate = {"i": 0, "nobar": False}

        def next_w():
            i = wstate["i"]
            wstate["i"] = 1 - i
            return wbufs[i], ("W", i)

        def load_w(dst3, key, src2d, nk, ncols):
            DMA("pool", dst3[:, 0:nk, 0:ncols], src2d.rearrange("(k p) c -> p k c", p=128), [], [key])

        def norm_stA(it):
            col = (it % 2) * 32 + 16
            ss = stat2[:, col:col + 1]
            P.op("act", lambda e, it=it, ss=ss: e.activation(out=junk[:, :], in_=x[:, it, :], func=AF.Square,
                                                          accum_out=ss), [("x", it)], [("s2", col)])
            RSTD(ss, float(D), ("s2", col))

        def norm_stB(it, gcol0):
            col = (it % 2) * 32 + 16
            ss = stat2[:, col:col + 1]
            for hf in range(2):
                xbh = stgB[hf]
                TS("dve", xbh[:, :], x[:, it, hf * 512:(hf + 1) * 512], ss, None, ALU.mult, None,
                   [("x", it), ("s2", col)], [("stgB", hf)])
                for c4 in range(4):
                    kc = hf * 4 + c4
                    TRP(PS[c4][:, 0:128], xbh[:, c4 * 128:(c4 + 1) * 128], ident[:],
                        [("stgB", hf)], [("ps", c4)])
                    if c4 % 2 == 0:
                        TS("dve", hT[:, kc, it * 128:(it + 1) * 128], PS[c4][:, 0:128],
                           gT[:, gcol0 + kc:gcol0 + kc + 1], None, ALU.mult, None,
                           [("ps", c4), "gT"], [("hT", it // 4)])
                    else:
                        P.op("act", lambda e, kc=kc, it=it, c4=c4: e.activation(
                            out=hT[:, kc, it * 128:(it + 1) * 128], in_=PS[c4][:, 0:128], func=AF.Copy,
                            scale=gT[:, gcol0 + kc:gcol0 + kc + 1]), [("ps", c4), "gT"], [("hT", it // 4)])

        def norm_to_hT(l, gcol0):
            norm_stA(0)
            for it in range(NT):
                if it + 1 < NT:
                    norm_stA(it + 1)
                norm_stB(it, gcol0)

        def proj_T(wsrc_list, ncols, consumer, lhs_fn=None, nk=8, psel=(0, 1)):
            wb, wkey = next_w()
            c0 = 0
            for src, w in wsrc_list:
                DMA("pool", wb[:, 0:nk, c0:c0 + w], src.rearrange("(k p) c -> p k c", p=128), [], [wkey],
                    nobar=wstate["nobar"])
                c0 += w
            assert c0 == ncols and ncols <= 512
            pend = None
            for it in range(NT):
                pi = psel[it % len(psel)]
                ps = PS[pi]
                for kc in range(nk):
                    lhs = lhs_fn(kc, it) if lhs_fn else hT[:, kc, it * 128:(it + 1) * 128]
                    rd = [wkey, ("hT", it // 4)] if lhs_fn is None else [wkey, "lhs"]
                    MM(ps[:, 0:ncols], lhs, wb[:, kc, 0:ncols], kc == 0, kc == nk - 1, rd, [("ps", pi)])
                nxt = consumer(it, ps[:, 0:ncols], ("ps", pi))
                if pend is not None:
                    pend()
                pend = nxt
            if pend is not None:
                pend()

        def proj_F(wsegs, m, consumer, psel=(0, 1)):
            wb, wkey = next_w()
            c0 = 0
            for src, w in wsegs:
                DMA("pool", wb[:, 0:8, c0:c0 + w], src.rearrange("(k p) c -> p k c", p=128), [], [wkey],
                    nobar=wstate["nobar"])
                c0 += w
            assert c0 == m
            for tc in range(4):
                pi = psel[tc % len(psel)]
                ps = PS[pi]
                for kc in range(8):
                    MM(ps[0:m, :], wb[:, kc, 0:m], hT[:, kc, tc * 512:(tc + 1) * 512], kc == 0, kc == 7,
                       [wkey, ("hT", tc)], [("ps", pi)])
                consumer(tc, ps[0:m, :], ("ps", pi))

        def head_norm(src3, nh, w, gain2, out3, rkeys, wkeys, statcol, post=None):
            sq = tmpF[:, 0:nh * w].rearrange("p (h c) -> p h c", h=nh)
            ss = stat[:, statcol:statcol + nh]
            skey = ("stat", statcol)
            ACT(sq, src3, AF.Square, rkeys, ["tmpF"])
            RED(ss, sq, ["tmpF"], [skey])
            RSTD(ss, float(w), skey)
            TT("dve", sq, src3, bc(ss.unsqueeze(2), [128, nh, w]), ALU.mult, rkeys + [skey, "tmpF"], ["tmpF"])
            TT("dve", out3, sq, bc(gain2.unsqueeze(1), [128, nh, w]), ALU.mult, ["tmpF", "gb"], wkeys)

        ystc = {"i": 0}

        def y_out(row0, qc, src_fn):
            i = ystc["i"]
            ystc["i"] = 1 - i
            dst = yst[i][0:64, :]
            src_fn(dst, ("yst", i))
            DMA("sp", yT_d[row0:row0 + 64, qc * 512:(qc + 1) * 512], dst, [("yst", i)], [("yT", row0, qc)])

        ptc = {"i": 0}

        def softmax_attn(qc, tiles, score_fn, v_fn, obank, act_bias_fn=None, sbanks=(0, 1, 2, 3), la=2):
            nt = len(tiles)
            nb = len(sbanks)

            def emit_score(n):
                stl = tiles[n]
                j0 = max(0, stl - 4 * qc) * 128
                si = sbanks[n % nb]
                score_fn(stl, j0, PS[si], ("ps", si))
            for n in range(min(la, nt)):
                emit_score(n)
            for n, stl in enumerate(tiles):
                if n + la < nt:
                    emit_score(n + la)
                j0 = max(0, stl - 4 * qc) * 128
                si = sbanks[n % nb]
                sps = PS[si]
                i = ptc["i"]
                ptc["i"] = 1 - i
                pt = PT[i]
                bias = act_bias_fn(stl) if act_bias_fn else None
                rk = [("ps", si)] + (["abias"] if bias is not None else [])
                ACT(pt[:, j0:512], sps[:, j0:512], AF.Exp, rk, [("PT", i)], bias=bias)
                vl, vkey = v_fn(stl)
                MM(PS[obank][:, j0:512], vl, pt[:, j0:512], n == 0, n == nt - 1,
                   [("PT", i), vkey], [("ps", obank)])

        def layer(l):
            DMA("sp", grow[0:8, :], g_mix_d[l].rearrange("(k p) -> k p", p=128), [], ["grow"])
            DMA("sp", grow[8:16, :], g_mlp_d[l].rearrange("(k p) -> k p", p=128), [], ["grow"])
            DMA("sp", grow[16:24, :], g_ple_d[l].rearrange("(k p) -> k p", p=128), [], ["grow"])
            DMA("sp", grow[24:27, :], cqg_d[l].rearrange("(k p) -> k p", p=128), [], ["grow"])
            DMA("sp", grow[27:28, :], ckvg_d[l].rearrange("(k p) -> k p", p=128), [], ["grow"])
            TRP(PS[0][:, 0:28], grow[0:28, :], identf[0:28, 0:28], ["grow"], [("ps", 0)])
            CP("dve", gT[:, 0:28], PS[0][:, 0:28], [("ps", 0)], ["gT"])
            DMA("sp", gb[:, 0:96], mqn_d[l:l + 1, :].to_broadcast([128, 96]), [], ["gb"])
            DMA("sp", gb[:, 96:192], mkn_d[l:l + 1, :].to_broadcast([128, 96]), [], ["gb"])
            DMA("sp", gb[:, 192:256], nqn_d[l:l + 1, :].to_broadcast([128, 64]), [], ["gb"])
            DMA("sp", gb[:, 256:448], nkn_d[l:l + 1].rearrange("a b c -> a (b c)").to_broadcast([128, 192]), [], ["gb"])
            DMA("sp", gb[:, 448:512], fqn_d[l:l + 1, :].to_broadcast([128, 64]), [], ["gb"])
            DMA("sp", gb[:, 512:576], fkn_d[l:l + 1, :].to_broadcast([128, 64]), [], ["gb"])
            DMA("sp", gb[:, 576:580], ffb_d[l:l + 1, :].to_broadcast([128, 4]), [], ["gb"])
            TS("dve", gb[:, 0:96], gb[:, 0:96], 96.0 ** -0.5, None, ALU.mult, None, ["gb"], ["gb"])
            TS("dve", gb[:, 192:256], gb[:, 192:256], 0.125, None, ALU.mult, None, ["gb"], ["gb"])
            TS("dve", gb[:, 448:512], gb[:, 448:512], 0.125, None, ALU.mult, None, ["gb"], ["gb"])
            P.barrier()
            chk("setup")
            norm_to_hT(l, 0)
            P.barrier()
            if stop in ("n_a", "n_b", "n_c", "n_d", "n_e"):
                raise _Stop()
            chk("norm1")
            wl = w_in_d[l]

            def sbq_cons(c):
                def f(tc, ps, pk):
                    CP("act", QA[:, c, tc * 512:(tc + 1) * 512], ps, [pk], [("QA", c, tc)])
                return f

            def sbk_cons(c):
                def f(tc, ps, pk):
                    P.op("act", lambda e: e.activation(out=KA[:, c, tc * 512:(tc + 1) * 512], in_=ps, func=AF.Copy,
                                                       scale=0.125), [pk], [("KA", c, tc)])
                return f
            for c in range(2):
                proj_F([(wl[:, c * 128:(c + 1) * 128], 128)], 128, sbq_cons(c))
                if c == 0:
                    wstate["nobar"] = True
                proj_F([(wl[:, 256 + c * 128:256 + (c + 1) * 128], 128)], 128, sbk_cons(c))

            def sbv_cons(it, ps, pk):
                CP("act", VA4[:, it, :, :], ps.rearrange("p (a c) -> p a c", a=4), [pk], [("VA", it)])
            proj_T([(wl[:, 512:768], 256)], 256, sbv_cons)
            psb_bank = (PSB, "psb")
            for c in range(2):
                for qc in range(4):
                    tl = list(range(4 * qc + 3, -1, -1))
                    nt = len(tl)
                    streams = []
                    for sI in range(2):
                        streams.append(dict(
                            h=c * 2 + sI, r0=sI * 64,
                            Z=((0, 1), (2, 3))[sI],
                            C=(4, 6)[sI],
                            O=((PS[5], ("ps", 5)), psb_bank)[sI],
                            e2=[KBf[:, (sI * 2 + j) * 512:(sI * 2 + j + 1) * 512] for j in range(2)],
                            lp=lpB[:, sI * 512:(sI + 1) * 512], lk=("lpB", sI),
                            pt=PT[sI], pk=("PT", sI)))
                    for S_ in streams:
                        MM(PS[S_["C"]][:, :], zeros[:, :], wc[:, :], True, False, ["zeros", "wc"], [("ps", S_["C"])])
                        MM(S_["O"][0][0:64, :], zeros[:, 0:64], wc[:, :], True, False, ["zeros", "wc"], [S_["O"][1]])

                    def geo(n):
                        stl = tl[n]
                        j0 = max(0, stl - 4 * qc) * 128
                        return stl, j0, stl >= 4 * qc, slice(stl * 128, (stl + 1) * 128), slice(qc * 512 + j0, (qc + 1) * 512)

                    def z_mm(S_, n):
                        stl, j0, diag, ks, qs = geo(n)
                        zi = S_["Z"][n % 2]
                        r0 = S_["r0"]
                        MM(PS[zi][:, j0:512], KA[r0:r0 + 64, c, ks], QA[r0:r0 + 64, c, qs], True, True,
                           [("QA", c, qc), ("KA", c, stl // 4)], [("ps", zi)])
                    for S_ in streams:
                        z_mm(S_, 0)
                    for n in range(nt):
                        stl, j0, diag, ks, qs = geo(n)
                        last = n == nt - 1
                        for sI_, S_ in enumerate(streams):
                            zi = S_["Z"][n % 2]
                            S_["e"] = S_["e2"][n % 2]
                            S_["ek"] = ("sbe", sI_, n % 2)
                            ACT(S_["e"][:, j0:512], PS[zi][:, j0:512], AF.Exp, [("ps", zi)], [S_["ek"]])
                            ACT(S_["lp"][:, j0:512], S_["e"][:, j0:512], AF.Ln, [S_["ek"]], [S_["lk"]], bias=1.0)
                            if diag:
                                TT("dve", S_["lp"][:, j0:512], S_["lp"][:, j0:512], wm01[:, 0:512 - j0], ALU.mult,
                                   [S_["lk"], "wm01"], [S_["lk"]])
                        if not last:
                            for S_ in streams:
                                z_mm(S_, n + 1)
                        for S_ in streams:
                            r0 = S_["r0"]
                            CB = S_["C"]
                            MM(PS[CB][:, j0:512], minc[:, :], S_["lp"][:, j0:512], False, False, [S_["lk"], "minc"], [("ps", CB)])
                        for S_ in streams:
                            CB = S_["C"]
                            ACT(S_["pt"][:, j0:512], PS[CB][:, j0:512], AF.Exp, [("ps", CB)], [S_["pk"]], scale=-1.0)
                            TT("dve", S_["pt"][:, j0:512], S_["pt"][:, j0:512], S_["e"][:, j0:512], ALU.mult,
                               [S_["pk"], S_["ek"]], [S_["pk"]])
                            if diag:
                                TT("dve", S_["pt"][:, j0:512], S_["pt"][:, j0:512], wm01[:, 0:512 - j0], ALU.mult,
                                   [S_["pk"], "wm01"], [S_["pk"]])
                        for S_ in streams:
                            r0 = S_["r0"]
                            CB = S_["C"]
                            if not last:
                                MM(PS[CB][:, j0:512], mcomp[:, :], S_["lp"][:, j0:512], False, False, [S_["lk"], "mcomp"], [("ps", CB)])
                            MM(S_["O"][0][0:64, j0:512], VA4[:, stl, S_["h"], :], S_["pt"][:, j0:512], False, last,
                               [S_["pk"], ("VA", stl)], [S_["O"][1]])
                    for S_ in streams:
                        y_out(0 + S_["h"] * 64, qc,
                              lambda dst, wk, S_=S_: CP("dve", dst, S_["O"][0][0:64, :], [S_["O"][1]], [wk]))
            P.barrier()

            chk("sb")
            cqT = XB[:, :].rearrange("p (a t) -> p a t", a=3)
            ckvT = KB[:, 0, :]
            DMA("sp", cossin, cd["cossin_f"], [], ["cossin"])

            def mla_c_cons_a(it, ps, pk):
                base = (it % 2) * 32 + 20
                skeys = ssq_A([ps[:, 0:384], ps[:, 384:512]], base, [pk])
                RSTD(stat2[:, base:base + 1], 384.0, skeys[0])
                RSTD(stat2[:, base + 1:base + 2], 128.0, skeys[1])

                def stB():
                    for (o, w, nm, j) in ((0, 384, "q", 0), (384, 128, "kv", 1)):
                        ss = stat2[:, base + j:base + j + 1]
                        sg = stgB[j]
                        sgk = ("stgB", j)
                        TS("dve", sg[:, 0:w], ps[:, o:o + w], ss, None, ALU.mult, None, [pk, skeys[j]], [sgk])
                        for cc in range(w // 128):
                            tb, tk = TB[cc if nm == "q" else 3]
                            TRP(tb[:, 0:128], sg[:, cc * 128:(cc + 1) * 128], ident[:], [sgk], [tk])
                            if nm == "q":
                                TS("dve", cqT[:, cc, it * 128:(it + 1) * 128], tb[:, 0:128],
                                   gT[:, 24 + cc:25 + cc], None, ALU.mult, None, [tk, "gT"], ["cqT"])
                            else:
                                TS("dve", ckvT[:, it * 128:(it + 1) * 128], tb[:, 0:128],
                                   gT[:, 27:28], None, ALU.mult, None, [tk, "gT"], ["ckvT"])
                return stB
            proj_T([(wl[:, 768:1280], 512)], 512, mla_c_cons_a)

            def mla_kr_cons(it, ps, pk):
                CP("act", krope[:, it, :], ps, [pk], [("krope", it)])
                P.op("act", lambda e, it=it: e.activation(out=junk[:, 0:32], in_=ps, func=AF.Square,
                                                        accum_out=krb[:, it:it + 1]), [pk], ["krb"])
                TS("dve", krb[:, it:it + 1], krb[:, it:it + 1], 1.0 / 96.0, EPS, ALU.mult, ALU.add, ["krb"], ["krb"])
            proj_T([(wl[:, 1280:1312], 32)], 32, mla_kr_cons)
            P.barrier()
            for pr in range(2):
                def rope_and_T(sq, sqk, it, dstbuf, pr, nm, gain):
                    sgb = stgB[0] if nm == "QA" else stgB[1]
                    sgk = ("stgB", 0 if nm == "QA" else 1)
                    sg = sgb[:, 0:192].rearrange("p (h c) -> p h c", h=2)
                    TT("dve", sg, sq, bc(gain.unsqueeze(1), [128, 2, 96]), ALU.mult, [sqk, "gb"], [sgk])
                    o0 = 704 if nm == "QA" else 832
                    rk_ = ("tmpFc", nm)
                    A_ = tmpF[:, o0:o0 + 64].rearrange("p (h a c) -> p h a c", h=2, a=2)
                    B_ = tmpF[:, o0 + 64:o0 + 128].rearrange("p (h a c) -> p h a c", h=2, a=2)
                    r4 = sg[:, :, 64:96].rearrange("p h (a c) -> p h a c", a=2)
                    csb = bc(cossin[:, it, 0, :].unsqueeze(1).unsqueeze(1), [128, 2, 2, 16])
                    snb = bc(cossin[:, it, 1, :].unsqueeze(1).unsqueeze(1), [128, 2, 2, 16])
                    TT("dve", A_, r4, csb, ALU.mult, [sgk, "cossin"], [rk_])
                    TT("dve", B_, r4, snb, ALU.mult, [sgk, "cossin"], [rk_])
                    TT("dve", sg[:, :, 64:80], A_[:, :, 0, :], B_[:, :, 1, :], ALU.subtract, [rk_], [sgk])
                    TT("dve", sg[:, :, 80:96], A_[:, :, 1, :], B_[:, :, 0, :], ALU.add, [rk_], [sgk])
                    for hh in range(2):
                        tb, tk = TB[(2 if nm == "KA" else 0) + hh]
                        TRP(tb[0:96, 0:128], sg[:, hh, :], ident[:], [sgk], [tk])
                        CP("act", dstbuf[0:96, hh, it * 128:(it + 1) * 128], tb[0:96, 0:128],
                           [tk], [(nm, hh, it // 4)])

                def mq_cons(it, ps, pk, pr=pr):
                    base = (it % 2) * 32 + 12
                    q3 = ps.rearrange("p (h c) -> p h c", h=2)
                    skeys = ssq_A([ps[:, 0:96], ps[:, 96:192]], base, [pk])
                    RSTD(stat2[:, base:base + 2], 96.0, skeys)

                    def stB():
                        sq = tmpF[:, 0:192].rearrange("p (h c) -> p h c", h=2)
                        ss = stat2[:, base:base + 2]
                        TT("dve", sq, q3, bc(ss.unsqueeze(2), [128, 2, 96]), ALU.mult, [pk] + skeys, ["tmpFa"])
                        rope_and_T(sq, "tmpFa", it, QA, pr, "QA", gb[:, 0:96])
                    return stB
                wsrc = wuq_d[l][:, pr * 192:(pr + 1) * 192]
                proj_T([(wsrc, 192)], 192, mq_cons, lhs_fn=lambda kc, it: cqT[:, kc, it * 128:(it + 1) * 128], nk=3)

                def mkv_cons(it, ps, pk, pr=pr):
                    kv3 = ps.rearrange("p (h c) -> p h c", h=2)
                    base = (it % 2) * 32 + 24
                    skeys = ssq_A([ps[:, 0:64], ps[:, 128:192]], base, [pk])
                    ss = stat2[:, base:base + 2]
                    ACT(ss, ss, AF.Ln, skeys + ["krb"], skeys, bias=krb[:, it:it + 1], scale=1.0 / 96.0)
                    ACT(ss, ss, AF.Exp, skeys, skeys, scale=-0.5)

                    def stB():
                        CP("act", VA[:, it, :, 0:64], kv3[:, :, 64:128], [pk], [("VA", it)])
                        sq = tmpF[:, 192:384].rearrange("p (h c) -> p h c", h=2)
                        TT("dve", sq[:, :, 0:64], kv3[:, :, 0:64], bc(ss.unsqueeze(2), [128, 2, 64]), ALU.mult,
                           [pk] + skeys, ["tmpFk"])
                        TT("dve", sq[:, :, 64:96], bc(krope[:, it, :].unsqueeze(1), [128, 2, 32]),
                           bc(ss.unsqueeze(2), [128, 2, 32]), ALU.mult, [("krope", it)] + skeys, ["tmpFk"])
                        rope_and_T(sq, "tmpFk", it, KA, pr, "KA", gb[:, 96:192])
                    return stB
                wsrc = wukv_d[l][:, pr * 256:(pr + 1) * 256]
                proj_T([(wsrc, 256)], 256, mkv_cons, lhs_fn=lambda kc, it: ckvT[:, it * 128:(it + 1) * 128], nk=1)
                if pr == 0:
                    MEMSET("dve", VA[:, :, :, 64:128], 1.0, [("VA", it_) for it_ in range(NT)])
                for hh in range(2):
                    h = pr * 2 + hh
                    for qc in range(4):
                        def sc(stl, j0, sps, pk, hh=hh, qc=qc):
                            diag = stl >= 4 * qc
                            MM(sps[:, j0:512], KA[0:96, hh, stl * 128:(stl + 1) * 128],
                               QA[0:96, hh, qc * 512 + j0:(qc + 1) * 512], True, not diag,
                               [("KA", hh, stl // 4), ("QA", hh, qc)], [pk])
                            if diag:
                                MM(sps[:, j0:512], ident[:, :], wc[:, 0:512 - j0], False, True, ["ident", "wc"], [pk])
                        softmax_attn(qc, list(range(4 * qc + 4)), sc,
                                     lambda stl, hh=hh: (VA[:, stl, hh, :], ("VA", stl)), 5)

                        def fin(dst, wk):
                            RECIP(tmpF[0:64, 0:512], PS[5][64:128, :], [("ps", 5)], ["tmpFa"])
                            TT("dve", dst, PS[5][0:64, :], tmpF[0:64, 0:512], ALU.mult, [("ps", 5), "tmpFa"], [wk])
                        y_out(256 + h * 64, qc, fin)
                P.barrier()

            chk("mla")
            sgT = KB[0:12, 0, :]
            kcT2 = XB[:, 0:2048]
            vcT2 = XB[:, 2048:4096]
            w1k = XB[:, 4096:5120].rearrange("p (c j) -> p c j", c=16)
            w1v = XB[:, 5120:6144].rearrange("p (c j) -> p c j", c=16)
            w2k = smallb[0:64, 0:64]
            w2v = smallb[0:64, 64:128]
            pekT = smallb[:, 128:144]
            pevT = smallb[:, 144:160]
            kcTd = nsasm[:, 0:127]
            vcx = nsasm[0:127, 128:256]
            DMA("pool", w1k, w1k_d[l].rearrange("(c p) j -> p c j", p=128), [], ["w1k"])
            DMA("pool", w1v, w1v_d[l].rearrange("(c p) j -> p c j", p=128), [], ["w1v"])
            DMA("pool", w2k, w2k_d[l], [], ["w2"])
            DMA("pool", w2v, w2v_d[l], [], ["w2"])
            DMA("sp", grow[0:16, :], pek_d[l].rearrange("(c a) d -> c (a d)", a=2), [], ["grow"])
            DMA("sp", grow[16:32, :], pev_d[l].rearrange("(c a) d -> c (a d)", a=2), [], ["grow"])
            TRP(PS[0][:, 0:32], grow[0:32, :], identf[0:32, 0:32], ["grow"], [("ps", 0)])
            CP("dve", smallb[:, 128:160], PS[0][:, 0:32], [("ps", 0)], ["peT"])
            for (w1, peT, col) in ((w1k, pekT, 40), (w1v, pevT, 41)):
                for cch in range(16):
                    MM(PS[0][0:64, 64 + (col - 40):65 + (col - 40)], w1[:, cch, :], peT[:, cch:cch + 1], cch == 0, cch == 15,
                       ["w1k", "w1v", "peT"], [("ps", 0)])
                CP("dve", stat[0:64, col:col + 1], PS[0][0:64, 64 + (col - 40):65 + (col - 40)], [("ps", 0)], [("stat", col)])

            def dup_cons(dst):
                def f(tc, ps, pk):
                    CP("act", dst[0:64, tc * 512:(tc + 1) * 512], ps[0:64, :], [pk], [("dupT", tc)])
                    if tc == 0:
                        CP("dve", dst[64:128, 0:511], ps[64:128, 1:512], [pk], [("dupT", tc)])
                    else:
                        CP("dve", dst[64:128, tc * 512 - 1:(tc + 1) * 512 - 1], ps[64:128, :], [pk], [("dupT", tc), ("dupT", tc - 1)])
                return f
            proj_F([(wl[:, 1568:1632], 64), (wl[:, 1568:1632], 64)], 128, dup_cons(kcT2))
            proj_F([(wl[:, 1632:1696], 64), (wl[:, 1632:1696], 64)], 128, dup_cons(vcT2))

            def g_cons(tc, ps, pk):
                ACT(tmpE[0:12, :], ps[0:12, :], AF.Exp, [pk], ["tmpE"], scale=-1.0)
                TS("dve", tmpE[0:12, :], tmpE[0:12, :], 1.0, None, ALU.add, None, ["tmpE"], ["tmpE"])
                RECIP(sgT[:, tc * 512:(tc + 1) * 512], tmpE[0:12, :], ["tmpE"], [("sgT", tc)])
            proj_F([(wl[:, 1952:1964], 12)], 12, g_cons)
            P.barrier()
            for (w1, srcT, col, w2, is_k) in ((w1k, kcT2, 40, w2k, True), (w1v, vcT2, 41, w2v, False)):
                for cch in range(16):
                    MM(PS[1][0:64, 0:127], w1[:, cch, :], srcT[:, 2 * cch:2 * cch + 16 * 126 + 1:16], cch == 0, cch == 15,
                       ["w1k", "w1v", ("dupT", 0)], [("ps", 1)])
                z = tmpF[0:64, 0:127]
                TS("dve", z, PS[1][0:64, 0:127], stat[0:64, col:col + 1], None, ALU.add, None, [("ps", 1), ("stat", col)], ["tmpF"])
                e = tmpF[0:64, 128:255]
                ACT(e, z, AF.Exp, ["tmpF"], ["tmpFe"], scale=-1.0)
                TS("dve", e, e, 1.0, None, ALU.add, None, ["tmpFe"], ["tmpFe"])
                RECIP(e, e, ["tmpFe"], ["tmpFe"])
                aTt = stgB[0][0:64, 0:127]
                TT("dve", aTt, z, e, ALU.mult, ["tmpF", "tmpFe"], [("stgB", 0)])
                MM(PS[1][0:127, 128:192], aTt, w2, True, True, [("stgB", 0), "w2"], [("ps", 1)])
                if is_k:
                    src3 = PS[1][0:127, 128:192].rearrange("p (h c) -> p h c", h=1)
                    sq = tmpF[0:127, 256:320].rearrange("p (h c) -> p h c", h=1)
                    ss = stat[0:127, 42:43]
                    sk = ("stat", 42)
                    ACT(sq, src3, AF.Square, [("ps", 1)], ["tmpFs"])
                    RED(ss, sq, ["tmpFs"], [sk])
                    RSTD(ss, 64.0, sk)
                    TS("dve", sq[:, 0, :], PS[1][0:127, 128:192], ss, None, ALU.mult, None, [("ps", 1), sk, "tmpFs"], ["tmpFs"])
                    sg = stgB[1][0:127, 0:128]
                    TT("dve", sg[:, 0:64], sq[:, 0, :], gb[0:127, 256:320], ALU.mult, ["tmpFs", "gb"], [("stgB", 1)])
                    CP("dve", sg[:, 64:128], sg[:, 0:64], [("stgB", 1)], [("stgB", 1)])
                    TRP(PSB[:, 0:127], sg, ident[0:127, 0:127], [("stgB", 1)], ["psb"])
                    CP("dve", kcTd, PSB[:, 0:127], ["psb"], ["kcTd"])
                else:
                    CP("dve", vcx[:, 0:64], PS[1][0:127, 128:192], [("ps", 1)], ["vcx"])
                    MEMSET("dve", vcx[:, 64:128], 1.0, ["vcx"])
            def nsa_cons(it, ps, pk):
                base = (it % 2) * 32
                srcs = [ps[:, h * 64:(h + 1) * 64] for h in range(4)] + [ps[:, 256:320], ps[:, 384:448]]
                skeys = ssq_A(srcs, base, [pk])
                RSTD(stat2[:, base:base + 6], 64.0, skeys)

                def stB():
                    q3 = ps[:, 0:256].rearrange("p (h c) -> p h c", h=4)
                    sg = stgB[0][:, 0:256].rearrange("p (h c) -> p h c", h=4)
                    norm_B(q3, 4, 64, gb[:, 192:256], sg, stat2[:, base:base + 4], skeys[0:4], [pk], [("stgB", 0)])
                    for c in range(2):
                        tb, tk = TB[c]
                        TRP(tb[:, 0:128], stgB[0][:, c * 128:(c + 1) * 128], ident[:], [("stgB", 0)], [tk])
                        CP("act", QA[:, c, it * 128:(it + 1) * 128], tb[:, 0:128], [tk], [("QA", c, it // 4)])
                    for (o, gi, slot) in ((256, 1, 0), (384, 2, 1)):
                        k3 = ps[:, o:o + 64].rearrange("p (h c) -> p h c", h=1)
                        sgk = stgB[1][:, slot * 128:slot * 128 + 128]
                        norm_B(k3, 1, 64, gb[:, 256 + gi * 64:320 + gi * 64],
                               sgk[:, 0:64].rearrange("p (h c) -> p h c", h=1),
                               stat2[:, base + 4 + slot:base + 5 + slot], [skeys[4 + slot]], [pk], [("stgB", 1)])
                        CP("dve", sgk[:, 64:128], sgk[:, 0:64], [("stgB", 1)], [("stgB", 1)])
                        tb, tk = TB[2 + slot]
                        TRP(tb[:, 0:128], sgk, ident[:], [("stgB", 1)], [tk])
                        CP("act", KA[:, slot, it * 128:(it + 1) * 128], tb[:, 0:128], [tk], [("KA", slot, it // 4)])
                    CP("dve", VA[:, it, 0, 0:64], ps[:, 320:384], [pk], [("VA", it)])
                    CP("dve", VA[:, it, 1, 0:64], ps[:, 448:512], [pk], [("VA", it)])
                return stB
            proj_T([(wl[:, 1312:1568], 256), (wl[:, 1696:1952], 256)], 512, nsa_cons)
            P.barrier()
            chk("nsa_proj")
            MEMSET("dve", impacc[:], 0.0, ["impacc"])
            for h in range(4):
                c, r0 = h // 2, (h % 2) * 64
                for qc in range(4):
                    for q4 in range(4):
                        qt = qc * 4 + q4
                        si = 2 + (q4 % 2)
                        sps = PS[si]
                        MM(sps[0:127, 0:128], kcTd[r0:r0 + 64, :], QA[r0:r0 + 64, c, qt * 128:(qt + 1) * 128], True, False,
                           ["kcTd", ("QA", c, qc)], [("ps", si)])
                        MM(sps[0:127, 0:128], selm[:, qt, :], pat[:, h, :], False, True, ["selm", "pat"], [("ps", si)])
                        i = ptc["i"]
                        ptc["i"] = 1 - i
                        pt = PT[i]
                        ACT(pt[0:127, 0:128], sps[0:127, 0:128], AF.Exp, [("ps", si)], [("PT", i)])
                        MM(PS[5][:, q4 * 128:(q4 + 1) * 128], vcx, pt[0:127, 0:128], True, True, [("PT", i), "vcx"], [("ps", 5)])
                        if qt >= 8:
                            ef = tmpF[0:127, 512:640]
                            ACT(ef, sps[0:127, 0:128], AF.Exp, [("ps", si)], ["tmpFi"])
                            MM(PS[6][:, 0:34], ef, ovx[:, :], True, True, ["tmpFi", "ovx"], [("ps", 6)])
                            rs = stat[:, 52:53]
                            TS("dve", rs, PS[6][:, 32:33], 1e-30, None, ALU.add, None, [("ps", 6)], [("stat", 52)])
                            RECIP(rs, rs, [("stat", 52)], [("stat", 52)])
                            P.op("dve", lambda e, qt=qt, rs=rs: e.scalar_tensor_tensor(
                                out=impacc[:, qt - 8, :], in0=PS[6][:, 0:32], scalar=rs, in1=impacc[:, qt - 8, :],
                                op0=ALU.mult, op1=ALU.add), [("ps", 6), ("stat", 52), "impacc"], ["impacc"])
                    MM(PS[6][0:64, :], esel[:, 0 * 4 + h, :], sgT[:, qc * 512:(qc + 1) * 512], True, True,
                       ["esel", ("sgT", qc)], [("ps", 6)])

                    def fin(dst, wk):
                        rk = [("ps", 5)]
                        TS("dve", tmpE[0:64, :], PS[5][64:128, :], 1e-30, None, ALU.add, None, rk, ["tmpE"])
                        RECIP(tmpE[0:64, :], tmpE[0:64, :], ["tmpE"], ["tmpE"])
                        TT("dve", tmpE[0:64, :], tmpE[0:64, :], PS[6][0:64, :], ALU.mult, ["tmpE", ("ps", 6)], ["tmpE"])
                        TT("dve", dst, PS[5][0:64, :], tmpE[0:64, :], ALU.mult, rk + ["tmpE"], [wk])
                    y_out(512 + h * 64, qc, fin)
            P.barrier()
            chk("nsa_n1")
            MEMSET("dve", lpB[:, :], 0.0, ["selbT"])
            for q8 in range(8):
                iv = tmpF[:, q8 * 32:(q8 + 1) * 32]
                TT("dve", iv, impacc[:, q8, :], m2[:, q8, :], ALU.add, ["impacc", "m2"], [("iv", q8)])
                m8a = stat[:, 0:8]
                m8b = stat[:, 8:16]
                wk2 = tmpF[:, 512 + q8 * 32:512 + (q8 + 1) * 32]
                P.op("dve", lambda e, iv=iv, m8a=m8a: e.max(out=m8a, in_=iv), [("iv", q8)], ["m8a"])
                P.op("dve", lambda e, iv=iv, m8a=m8a, wk2=wk2: e.match_replace(out=wk2, in_to_replace=m8a, in_values=iv,
                                                                             imm_value=-3e9), [("iv", q8), "m8a"], [("wk2", q8)])
                P.op("dve", lambda e, wk2=wk2, m8b=m8b: e.max(out=m8b, in_=wk2), [("wk2", q8)], ["m8b"])
                sbm = stgB[q8 % 2][:, 0:32]
                TS("dve", sbm, iv, m8b[:, 7:8], NEGM, ALU.is_lt, ALU.mult, [("iv", q8), "m8b"], [("stgB", q8 % 2)])
                TRP(PSB[0:32, (q8 % 2) * 128:(q8 % 2 + 1) * 128], sbm, ident[:], [("stgB", q8 % 2)], ["psb"])
                CP("act", selbT[0:32, q8 * 128:(q8 + 1) * 128], PSB[0:32, (q8 % 2) * 128:(q8 % 2 + 1) * 128],
                   ["psb"], ["selbT"])
            P.barrier()
            chk("nsa_n2")
            for h in range(4):
                c, r0 = h // 2, (h % 2) * 64
                for qc in range(4):
                    def sc_slc(stl, j0, sps, pk, h=h, c=c, r0=r0, qc=qc):
                        dl = 512 * qc - 128 * stl
                        near = dl <= 128
                        need_sel = qc >= 2
                        MM(sps[:, j0:512], KA[r0:r0 + 64, 0, stl * 128:(stl + 1) * 128],
                           QA[r0:r0 + 64, c, qc * 512 + j0:(qc + 1) * 512], True, not (near or need_sel),
                           [("KA", 0, stl // 4), ("QA", c, qc)], [pk])
                        if near:
                            MM(sps[:, j0:512], antij[:, :], hb[:, h, j0 + dl:512 + dl], False, not need_sel, ["antij", "hb"], [pk])
                        if need_sel:
                            MM(sps[:, j0:512], exm[:, stl * 128:(stl + 1) * 128],
                               selbT[:, (qc - 2) * 512 + j0:(qc - 1) * 512], False, True, ["exm", "selbT"], [pk])

                    def bias_slc(stl, h=h, qc=qc):
                        return None if 512 * qc - 128 * stl <= 128 else t31[:, h:h + 1]
                    softmax_attn(qc, list(range(4 * qc + 4)), sc_slc, lambda stl: (VA[:, stl, 0, :], ("VA", stl)), 4,
                                 act_bias_fn=bias_slc)
                    if stop == "n3_slc":
                        continue

                    def sc_win(stl, j0, sps, pk, h=h, c=c, r0=r0, qc=qc):
                        dl = 512 * qc - 128 * stl
                        near = dl <= 128
                        upper = dl >= 128
                        MM(sps[:, j0:512], KA[r0:r0 + 64, 1, stl * 128:(stl + 1) * 128],
                           QA[r0:r0 + 64, c, qc * 512 + j0:(qc + 1) * 512], True, not (near or upper),
                           [("KA", 1, stl // 4), ("QA", c, qc)], [pk])
                        if near:
                            MM(sps[:, j0:512], antij[:, :], hb[:, h, j0 + dl:512 + dl], False, not upper, ["antij", "hb"], [pk])
                        if upper:
                            MM(sps[:, j0:512], ident[:, :], wu[:, dl:dl + 512], False, True, ["ident", "wu"], [pk])
                    softmax_attn(qc, list(range(max(0, 4 * qc - 4), 4 * qc + 4)), sc_win,
                                 lambda stl: (VA[:, stl, 1, :], ("VA", stl)), 5, act_bias_fn=bias_slc)
                    if stop == "n3_win":
                        continue
                    prev = stgB[0][0:64, :]
                    DMA("sp", prev, yT_d[512 + h * 64:512 + (h + 1) * 64, qc * 512:(qc + 1) * 512],
                        [("yT", 512 + h * 64, qc)], [("stgB", 0)])
                    acc = tmpF[0:64, 0:512]
                    CP("dve", acc, prev, [("stgB", 0)], ["tmpF"])
                    if stop == "n3_c1":
                        continue
                    for (br, ob) in ((1, 4), (2, 5)):
                        MM(PS[6][0:64, :], esel[:, br * 4 + h, :], sgT[:, qc * 512:(qc + 1) * 512], True, True,
                           ["esel", ("sgT", qc)], [("ps", 6)])
                        RECIP(tmpE[0:64, :], PS[ob][64:128, :], [("ps", ob)], ["tmpE"])
                        TT("dve", tmpE[0:64, :], tmpE[0:64, :], PS[6][0:64, :], ALU.mult, ["tmpE", ("ps", 6)], ["tmpE"])
                        TT("dve", tmpF[0:64, 512:1024], PS[ob][0:64, :], tmpE[0:64, :], ALU.mult, [("ps", ob), "tmpE"], ["tmpF2"])
                        TT("dve", acc, acc, tmpF[0:64, 512:1024], ALU.add, ["tmpF", "tmpF2"], ["tmpF"])
                    if stop == "n3_c2":
                        continue
                    y_out(512 + h * 64, qc, lambda dst, wk: CP("dve", dst, tmpF[0:64, 0:512], ["tmpF"], [wk]))
            P.barrier()

            if stop in ("n3_slc", "n3_win", "n3_c1", "n3_c2"):
                raise _Stop()
            chk("nsa")
            lf = foxc[:, 0, :, :]
            ncum = foxc[:, 1, :, :]

            def ff_cons(it, ps, pk):
                t = stat[:, 56:60]
                TT("dve", t, ps, gb[:, 576:580], ALU.add, [pk, "gb"], [("stat", 56)])
                ACT(t, t, AF.Exp, [("stat", 56)], [("stat", 56)], scale=-1.0)
                ACT(t, t, AF.Ln, [("stat", 56)], [("stat", 56)], bias=1.0)
                TS("dve", lf[:, it, :], t, -1.0, None, ALU.mult, None, [("stat", 56)], [("lf", it)])
            proj_T([(wl[:, 2732:2736], 4)], 4, ff_cons)
            P.barrier()
            for it in range(NT):
                for j in range(it + 1):
                    MM(PS[0][:, it * 4:(it + 1) * 4], utri[:, :] if j == it else onesf[:, :], lf[:, j, :], j == 0, j == it,
                       [("lf", j)], [("ps", 0)])
            P.barrier()
            cumT = tmpF[:, 0:64].rearrange("p (a c) -> p a c", a=NT)
            CP("dve", cumT, PS[0][:, 0:64].rearrange("p (a c) -> p a c", a=NT), [("ps", 0)], ["cum"])
            TS("dve", ncum, cumT, -1.0, None, ALU.mult, None, ["cum"], ["ncum"])
            cpc = smallb[:, 0:192].rearrange("p (a h c) -> p a h c", a=NT, h=4)
            r1 = tmpF[:, 64:128].rearrange("p (a c) -> p a c", a=NT)
            hi_f = tmpF[:, 128:192].rearrange("p (a c) -> p a c", a=NT)
            for k in range(3):
                CP("dve", cpc[:, :, :, k], cumT if k == 0 else r1, ["cum", "r1"], ["cpc"])
                if k < 2:
                    CP("dve", hi_f, cpc[:, :, :, k], ["cpc"], ["hi_f"])
                    TT("dve", r1, cumT if k == 0 else r1, hi_f, ALU.subtract, ["cum", "r1", "hi_f"], ["r1"])
            P.barrier()
            for pr in range(2):
                def fox_cons(it, ps, pk, pr=pr):
                    base = (it % 2) * 32 + 8
                    srcs = [ps[:, j * 64:(j + 1) * 64] for j in range(4)]
                    skeys = ssq_A(srcs, base, [pk])
                    RSTD(stat2[:, base:base + 4], 64.0, skeys)

                    def stB():
                        for (o, g0, dst, nm) in ((0, 448, QA, "QA"), (128, 512, KA, "KA")):
                            s3 = ps[:, o:o + 128].rearrange("p (h c) -> p h c", h=2)
                            sgi = 0 if nm == "QA" else 1
                            sg = stgB[sgi][:, 0:134].rearrange("p (h c) -> p h c", h=2)
                            norm_B(s3, 2, 64, gb[:, g0:g0 + 64], sg[:, :, 0:64], stat2[:, base + 2 * sgi:base + 2 * sgi + 2],
                                   skeys[2 * sgi:2 * sgi + 2], [pk], [("stgB", sgi)])
                            if nm == "QA":
                                CP("dve", sg[:, :, 64:67], cpc[:, it, pr * 2:pr * 2 + 2, :], ["cpc"], [("stgB", sgi)])
                            else:
                                MEMSET("dve", sg[:, :, 64:67], 1.0, [("stgB", sgi)])
                            for hh in range(2):
                                tb, tk = TB[sgi * 2 + hh]
                                TRP(tb[0:67, 0:128], sg[:, hh, :], ident[:], [("stgB", sgi)], [tk])
                                CP("act", dst[0:67, hh, it * 128:(it + 1) * 128], tb[0:67, 0:128],
                                   [tk], [(nm, hh, it // 4)])
                        CP("dve", VA[:, it, :, 0:64], ps[:, 256:384].rearrange("p (h c) -> p h c", h=2), [pk], [("VA", it)])
                    return stB
                proj_T([(wl[:, 1964 + pr * 128:1964 + (pr + 1) * 128], 128),
                        (wl[:, 2220 + pr * 128:2220 + (pr + 1) * 128], 128),
                        (wl[:, 2476 + pr * 128:2476 + (pr + 1) * 128], 128)], 384, fox_cons)
                for hh in range(2):
                    h = pr * 2 + hh
                    for qc in range(4):
                        def sc(stl, j0, sps, pk, hh=hh, qc=qc):
                            diag = stl >= 4 * qc
                            MM(sps[:, j0:512], KA[0:67, hh, stl * 128:(stl + 1) * 128],
                               QA[0:67, hh, qc * 512 + j0:(qc + 1) * 512], True, not diag,
                               [("KA", hh, stl // 4), ("QA", hh, qc)], [pk])
                            if diag:
                                MM(sps[:, j0:512], ident[:, :], wc[:, 0:512 - j0], False, True, ["ident", "wc"], [pk])
                        softmax_attn(qc, list(range(4 * qc + 4)), sc,
                                     lambda stl, hh=hh: (VA[:, stl, hh, :], ("VA", stl)), 5,
                                     act_bias_fn=lambda stl, h=h: ncum[:, stl, h:h + 1])

                        def fin(dst, wk):
                            RECIP(tmpE[0:64, :], PS[5][64:128, :], [("ps", 5)], ["tmpE"])
                            TT("dve", dst, PS[5][0:64, :], tmpE[0:64, :], ALU.mult, [("ps", 5), "tmpE"], [wk])
                        y_out(768 + h * 64, qc, fin)
                P.barrier()

            wstate["nobar"] = False
            chk("fox")
            ynR = XB[:, 0:4096].rearrange("p (s a t) -> p s a t", s=4, a=2)
            wsl = [W0[:, s_ * 1280:(s_ + 1) * 1280].rearrange("p (k c) -> p k c", k=10) for s_ in range(2)]
            sgs = [tmpE[:, :], tmpF[:, 512:1024]]
            prods = [tmpF[:, 0:512], lpB[:, :].bitcast(F32)]
            items = [(dc, n) for dc in range(8) for n in range(4)]

            def gate_load(i):
                dc, n = items[i]
                ws = wsl[i % 2]
                wk = ("wsl", i % 2)
                gcol = 2736 + n * 1024 + dc * 128
                DMA("pool", ws[:, 0:8, :], wl[:, gcol:gcol + 128].rearrange("(k p) c -> p k c", p=128), [], [wk])
                DMA("pool", ws[:, 8:10, :], wbr_d[l, n][:, dc * 128:(dc + 1) * 128].rearrange("(k p) c -> p k c", p=128), [], [wk])
            gate_load(0)
            cnt = 0
            for i, (dc, n) in enumerate(items):
                if i + 1 < len(items):
                    gate_load(i + 1)
                ws = wsl[i % 2]
                wk = ("wsl", i % 2)
                for tc in range(4):
                    DMA("sp", ynR[:, tc, :, :],
                        yT_d[n * 256:(n + 1) * 256, tc * 512:(tc + 1) * 512].rearrange("(k p) t -> p k t", p=128),
                        [], [("ynT", tc)])
                    gi = 2 + (tc % 2)
                    wi = 4 + (tc % 2)
                    for kc in range(8):
                        MM(PS[gi][:, :], ws[:, kc, :], hT[:, kc, tc * 512:(tc + 1) * 512], kc == 0, kc == 7,
                           [wk, ("hT", tc)], [("ps", gi)])
                    for kc in range(2):
                        MM(PS[wi][:, :], ws[:, 8 + kc, :], ynR[:, tc, kc, :], kc == 0, kc == 1,
                           [wk, ("ynT", tc)], [("ps", wi)])
                    sg = sgs[cnt % 2]
                    sgk = ("sg", cnt % 2)
                    pr = prods[cnt % 2]
                    prk = ("prod", cnt % 2)
                    cnt += 1
                    ACT(sg, PS[gi][:, :], AF.Sigmoid, [("ps", gi)], [sgk])
                    ua = uacc[:, tc * 512:(tc + 1) * 512]
                    if n == 0:
                        TT("dve", ua, PS[wi][:, :], sg, ALU.mult, [("ps", wi), sgk], [("uacc", tc)])
                    else:
                        TT("dve", pr, PS[wi][:, :], sg, ALU.mult, [("ps", wi), sgk], [prk])
                        TT("pool", ua, ua, pr, ALU.add, [("uacc", tc), prk], [("uacc", tc)])
                if n == 3:
                    for tc in range(4):
                        CP("act", uT[:, dc, tc * 512:(tc + 1) * 512], uacc[:, tc * 512:(tc + 1) * 512], [("uacc", tc)], [("uT", tc)])
            P.barrier()
            chk("gate")
            obanks = [(PS[4], ("ps", 4)), (PS[5], ("ps", 5)), (PS[6], ("ps", 6)), (PSB, "psb")]
            wok = [W0k, W1k]
            for hf in range(2):
                DMA("pool", wok[hf][:, :, :], wo_d[l][:, hf * 512:(hf + 1) * 512].rearrange("(k p) c -> p k c", p=128),
                    [], [("Wo", hf)])
            pendn = None
            for it in range(NT):
                for hf in range(2):
                    ob, okk = obanks[(it % 2) * 2 + hf]
                    for kc in range(8):
                        MM(ob[:, :], uT[:, kc, it * 128:(it + 1) * 128], wok[hf][:, kc, :], kc == 0, kc == 7,
                           [("Wo", hf), ("uT", it // 4)], [okk])
                    TT("dve", x[:, it, hf * 512:(hf + 1) * 512], ob[:, :], x[:, it, hf * 512:(hf + 1) * 512], ALU.add,
                       [okk, ("x", it)], [("x", it)])
                norm_stA(it)
                if pendn is not None:
                    norm_stB(pendn, 8)
                pendn = it
            norm_stB(pendn, 8)
            pT = XB[:, 0:4096].rearrange("p (a t) -> p a t", a=2)
            for it in range(NT):
                pst = yst[it % 2][:, 0:256]
                DMA("pool", pst, p_d[l][it * 128:(it + 1) * 128, :], [], [("yst", it % 2)])
                for c in range(2):
                    tb, tk = obanks[2 + c]
                    TRP(tb[:, 0:128], pst[:, c * 128:(c + 1) * 128], ident[:], [("yst", it % 2)], [tk])
                    CP("act", pT[:, c, it * 128:(it + 1) * 128], tb[:, 0:128], [tk], [("pT", it)])
            wpp = XB[:, 4096:6144].rearrange("p (k c) -> p k c", k=2)
            DMA("pool", wpp, wpp_d[l].rearrange("(k p) c -> p k c", p=128), [], ["wpp"])
            P.barrier()

            chk("wo")
            wdv = [W0[:, :].rearrange("p (k c) -> p k c", k=4), None]
            for g in range(8):
                DMA("pool", W1k[:, :, :], wup_d[l][:, g * 512:(g + 1) * 512].rearrange("(k p) c -> p k c", p=128), [], ["Wup"])
                DMA("pool", wdv[0], wdn_d[l][g * 512:(g + 1) * 512, :].rearrange("(k p) c -> p k c", p=128), [], ["Wdn"])
                for c4 in range(4):
                    for tc in range(4):
                        pi = 2 + ((c4 * 4 + tc) % 2)
                        for kc in range(8):
                            MM(PS[pi][:, :], W1k[:, kc, c4 * 128:(c4 + 1) * 128], hT[:, kc, tc * 512:(tc + 1) * 512],
                               kc == 0, kc == 7, ["Wup", ("hT", tc)], [("ps", pi)])
                        ACT(tmpE[:, :], PS[pi][:, :], AF.Square, [("ps", pi)], ["tmpE"])
                        P.op("dve", lambda e, pi=pi, c4=c4, tc=tc: e.scalar_tensor_tensor(
                            out=aT[:, c4, tc * 512:(tc + 1) * 512], in0=PS[pi][:, :], scalar=0.0, in1=tmpE[:, :],
                            op0=ALU.is_gt, op1=ALU.mult), [("ps", pi), "tmpE"], [("aT", c4, tc)])
                pendn = None
                for it in range(NT):
                    for hf in range(2):
                        ob, okk = obanks[(it * 2 + hf) % 4]
                        for c4 in range(4):
                            MM(ob[:, :], aT[:, c4, it * 128:(it + 1) * 128], wdv[0][:, c4, hf * 512:(hf + 1) * 512],
                               c4 == 0, c4 == 3, ["Wdn", ("aT", c4, it // 4)], [okk])
                        TT("dve", x[:, it, hf * 512:(hf + 1) * 512], ob[:, :], x[:, it, hf * 512:(hf + 1) * 512], ALU.add,
                           [okk, ("x", it)], [("x", it)])
                    if g == 7:
                        norm_stA(it)
                        if pendn is not None:
                            norm_stB(pendn, 16)
                        pendn = it
                if g == 7:
                    norm_stB(pendn, 16)
            P.barrier()

            chk("mlp")
            for hf in range(2):
                DMA("pool", W0k[:, :, :], wpg_d[l][:, hf * 512:(hf + 1) * 512].rearrange("(k p) c -> p k c", p=128), [], ["W0p"])
                for it in range(NT):
                    gi = 2 + it % 2
                    wi = 4 + it % 2
                    for kc in range(8):
                        MM(PS[gi][:, :], hT[:, kc, it * 128:(it + 1) * 128], W0k[:, kc, :], kc == 0, kc == 7,
                           ["W0p", ("hT", it // 4)], [("ps", gi)])
                    for kc in range(2):
                        MM(PS[wi][:, :], pT[:, kc, it * 128:(it + 1) * 128], wpp[:, kc, hf * 512:(hf + 1) * 512], kc == 0, kc == 1,
                           ["wpp", ("pT", it)], [("ps", wi)])
                    sg = [tmpE[:, :], tmpF[:, 512:1024]][it % 2]
                    sgk = ("sg", it % 2)
                    ACT(sg, PS[gi][:, :], AF.Sigmoid, [("ps", gi)], [sgk])
                    TT("dve", sg, PS[wi][:, :], sg, ALU.mult, [("ps", wi), sgk], [sgk])
                    TT("dve", x[:, it, hf * 512:(hf + 1) * 512], sg, x[:, it, hf * 512:(hf + 1) * 512], ALU.add,
                       [sgk, ("x", it)], [("x", it)])
            P.barrier()

        try:
            for l in range(n_layers):
                layer(l)
        except _Stop:
            P.barrier()
        fin_ops = []
        for it in range(NT):
            P.op("sp", lambda e, it=it: e.dma_start(out=out_d[it * 128:(it + 1) * 128, :], in_=x[:, it, :]),
                 [("x", it)], [], dma=True)
            fin_ops.append(len(P.ops) - 1)
        with nc.allow_low_precision(reason="bf16 matmul operands by design"):
            P.emit(st, final_wait_ops=fin_ops)
    return nc, consts


_CACHE = {}


def kernel(**inputs):
    if "nc" not in _CACHE:
        _CACHE["nc"] = build(DEPTH, False)
    nc, consts = _CACHE["nc"]
    shared = {}
    for k, v in inputs.items():
        if k in ("x", "p"):
            continue
        shared[k] = np.ascontiguousarray(np.asarray(v, dtype=np.float32))
    for k, v in consts.items():
        shared["c_" + k] = np.ascontiguousarray(v)
    xs = np.asarray(inputs["x"], dtype=np.float32)
    ps = np.asarray(inputs["p"], dtype=np.float32)
    in_maps = []
    for b in range(8):
        m = dict(shared)
        m["x"] = np.ascontiguousarray(xs[b])
        m["p"] = np.ascontiguousarray(ps[:, b])
        in_maps.append(m)
    res = run_bass_kernel_spmd(nc, in_maps, core_ids=list(range(8)))
    return np.stack([np.asarray(r["out"], dtype=np.float32) for r in res.results], axis=0)
```
